# Optimizing a Trainium2 kernel written in Bass

```python
import jax, jax.numpy as jnp
from jax import lax
import numpy as np

D_MODEL = 1024
BATCH = 4
SEQ = 4096
DEPTH = 4

CHUNK = 64
N_MIXERS = 2
EPS = 1e-6
GM_BLOCK = 128
GM_HEADS = 8
GM_WIDTH = 2 * D_MODEL
GM_HEAD_DIM = GM_WIDTH // GM_HEADS
HG_EXPAND = 128
HG_HEADS = D_MODEL // HG_EXPAND
HG_KEY = HG_EXPAND
HG_VAL = D_MODEL // HG_HEADS
FFN_HIDDEN = 2816
CONV_WIDTH = 3
N_A = (DEPTH + 1) // 2
N_B = DEPTH // 2

kernel_name = "hybrid_gmlp_hgrn2_convffn_adaln"


def rms_norm(x, g):
    xf = x.astype(jnp.float32)
    y = xf * lax.rsqrt(jnp.mean(xf * xf, axis=-1, keepdims=True) + EPS)
    return (y * g.astype(jnp.float32)).astype(x.dtype)


def chunk_causal_mask(n):
    idx = jnp.arange(n) // CHUNK
    return idx[:, None] >= idx[None, :]


def spatial_gating_mixer(h, w_in, ln_g, ln_b, w_s, b_s, w_out):
    bsz, t, _ = h.shape
    z = jax.nn.gelu(h @ w_in, approximate=False)
    u, v = jnp.split(z, 2, axis=-1)
    vf = v.astype(jnp.float32)
    mu = jnp.mean(vf, axis=-1, keepdims=True)
    var = jnp.mean(jnp.square(vf - mu), axis=-1, keepdims=True)
    v = ((vf - mu) * lax.rsqrt(var + EPS) * ln_g + ln_b).astype(h.dtype)
    v = v.reshape(bsz, t // GM_BLOCK, GM_BLOCK, GM_HEADS, GM_HEAD_DIM)
    ws = jnp.where(chunk_causal_mask(GM_BLOCK)[None], w_s, 0)
    s = jnp.einsum('hnm,bcmhd->bcnhd', ws, v) + b_s.T[None, None, :, :, None]
    gated = u * s.reshape(bsz, t, GM_WIDTH)
    return gated @ w_out


def hgrn2_chunked_scan(q, k, v, logf):
    bsz, t, nh, dk = q.shape
    dv = v.shape[-1]
    nc = t // CHUNK

    def to_chunks(a):
        return a.reshape(bsz, nc, CHUNK, nh, a.shape[-1]).transpose(1, 0, 3, 2, 4)

    causal = jnp.tril(jnp.ones((CHUNK, CHUNK), dtype=bool))[:, :, None]

    def step(state, inp):
        qc, kc, vc, gc = inp
        cum = jnp.cumsum(gc, axis=-2)
        rel = cum[..., :, None, :] - cum[..., None, :, :]
        decay = jnp.exp(jnp.where(causal, rel, -jnp.inf))
        scores = jnp.einsum('bhik,bhjk,bhijk->bhij', qc, kc, decay)
        out = jnp.einsum('bhij,bhjv->bhiv', scores, vc) + \
            jnp.einsum('bhik,bhkv->bhiv', qc * jnp.exp(cum), state)
        last = cum[..., -1:, :]
        state = jnp.exp(last)[..., 0, :, None] * state + \
            jnp.einsum('bhjk,bhjv->bhkv', kc * jnp.exp(last - cum), vc)
        return state, out

    s0 = jnp.zeros((bsz, nh, dk, dv), jnp.float32)
    _, out = lax.scan(step, s0, (to_chunks(q), to_chunks(k), to_chunks(v), to_chunks(logf)))
    return out.transpose(1, 0, 3, 2, 4).reshape(bsz, t, nh, dv)


def hgrn2_mixer(h, w_in, lb, gn_g, w_out):
    bsz, t, _ = h.shape
    q, fz, i, g = jnp.split(h @ w_in, 4, axis=-1)
    f = lb + (1.0 - lb) * jax.nn.sigmoid(fz.astype(jnp.float32))
    logf = jnp.log(f)
    k = 1.0 - f
    q = jax.nn.silu(q).astype(jnp.float32)

    def heads(a):
        return a.reshape(bsz, t, HG_HEADS, -1)

    o = hgrn2_chunked_scan(heads(q), heads(k), heads(i.astype(jnp.float32)), heads(logf))
    o = o * lax.rsqrt(jnp.mean(o * o, axis=-1, keepdims=True) + EPS)
    o = (o.reshape(bsz, t, HG_HEADS * HG_VAL) * gn_g.astype(jnp.float32)).astype(h.dtype)
    return (o * jax.nn.silu(g)) @ w_out


def conv_ffn(h, w_up, conv_w, conv_b, w_down):
    t = h.shape[1]
    a = h @ w_up
    ap = jnp.pad(a, ((0, 0), (CONV_WIDTH - 1, 0), (0, 0)))
    y = conv_b
    for j in range(CONV_WIDTH):
        y = y + conv_w[j] * ap[:, j:j + t]
    gate, val = jnp.split(y, 2, axis=-1)
    return (jax.nn.gelu(gate, approximate=False) * val) @ w_down


def setup_inputs(seed: int = 0) -> dict:
    key = jax.random.key(seed)
    ks = jax.random.split(key, 20)
    nrm = jax.random.normal
    f32 = jnp.float32
    D, F2 = D_MODEL, 2 * FFN_HIDDEN
    return {
        "x": nrm(ks[0], (BATCH, SEQ, D), f32),
        "c": nrm(ks[1], (BATCH, D), f32),
        "gm_w_in": nrm(ks[2], (N_A, D, 2 * GM_WIDTH), f32) * D ** -0.5,
        "gm_ln_g": 1.0 + 0.02 * nrm(ks[3], (N_A, GM_WIDTH), f32),
        "gm_ln_b": 0.02 * nrm(ks[4], (N_A, GM_WIDTH), f32),
        "gm_w_s": nrm(ks[5], (N_A, GM_HEADS, GM_BLOCK, GM_BLOCK), f32) * GM_BLOCK ** -0.5,
        "gm_b_s": 1.0 + 0.1 * nrm(ks[6], (N_A, GM_HEADS, GM_BLOCK), f32),
        "gm_w_out": nrm(ks[7], (N_A, GM_WIDTH, D), f32) * GM_WIDTH ** -0.5,
        "hg_w_in": nrm(ks[8], (N_B, D, 4 * D), f32) * D ** -0.5,
        "hg_lb": 0.5 * nrm(ks[9], (N_B, D), f32),
        "hg_gn_g": 1.0 + 0.02 * nrm(ks[10], (N_B, D), f32),
        "hg_w_out": nrm(ks[11], (N_B, D, D), f32) * D ** -0.5,
        "ffn_w_up": nrm(ks[12], (DEPTH, D, F2), f32) * D ** -0.5,
        "ffn_conv_w": nrm(ks[13], (DEPTH, CONV_WIDTH, F2), f32) * CONV_WIDTH ** -0.5,
        "ffn_conv_b": 0.02 * nrm(ks[14], (DEPTH, F2), f32),
        "ffn_w_down": nrm(ks[15], (DEPTH, FFN_HIDDEN, D), f32) * FFN_HIDDEN ** -0.5,
        "norm_g": 1.0 + 0.02 * nrm(ks[16], (DEPTH, 2, D), f32),
        "ada_w": nrm(ks[17], (DEPTH, D, 6 * D), f32) * D ** -0.5,
        "ada_b": 0.02 * nrm(ks[18], (DEPTH, 6 * D), f32),
        "final_g": 1.0 + 0.02 * nrm(ks[19], (D,), f32),
    }


def reference(x, c, gm_w_in, gm_ln_g, gm_ln_b, gm_w_s, gm_b_s, gm_w_out,
              hg_w_in, hg_lb, hg_gn_g, hg_w_out,
              ffn_w_up, ffn_conv_w, ffn_conv_b, ffn_w_down,
              norm_g, ada_w, ada_b, final_g):
    lb_p = jax.nn.softmax(hg_lb.astype(jnp.float32), axis=0)
    lb_all = jnp.cumsum(lb_p, axis=0) - lb_p[0]
    cond = jax.nn.silu(c)
    for i in range(DEPTH):
        mod = (cond @ ada_w[i] + ada_b[i])[:, None, :]
        sh1, sc1, g1, sh2, sc2, g2 = jnp.split(mod, 6, axis=-1)
        h = rms_norm(x, norm_g[i, 0]) * (1.0 + sc1) + sh1
        j = i // N_MIXERS
        if i % N_MIXERS == 0:
            y = spatial_gating_mixer(h, gm_w_in[j], gm_ln_g[j], gm_ln_b[j],
                                     gm_w_s[j], gm_b_s[j], gm_w_out[j])
        else:
            y = hgrn2_mixer(h, hg_w_in[j], lb_all[j], hg_gn_g[j], hg_w_out[j])
        x = x + g1 * y
        h = rms_norm(x, norm_g[i, 1]) * (1.0 + sc2) + sh2
        x = x + g2 * conv_ffn(h, ffn_w_up[i], ffn_conv_w[i], ffn_conv_b[i], ffn_w_down[i])
    return rms_norm(x, final_g)
```

```python
import contextlib
import numpy as np
import concourse.bass as bass
import concourse.mybir as mybir
from concourse.bass_utils import run_bass_kernel_spmd

F32 = mybir.dt.float32
BF16 = mybir.dt.bfloat16
U32 = mybir.dt.uint32
AF = mybir.ActivationFunctionType
ALU = mybir.AluOpType

D = 1024
T = 2048
DC = 8
FF = 2816
FC = 22
EPS = 1e-6
ENGS = ["pe", "act", "dve", "pool", "sp"]


class Prog:
    def __init__(self, nc):
        self.nc = nc
        self.ops = {e: [] for e in ENGS}
        self.cnt = {e: 0 for e in ENGS}
        self.known = {e: {} for e in ENGS}
        self.last_w = {}
        self.readers = {}
        self.dma_cnt = {}

    def _deps(self, eng, reads, writes):
        need = {}

        def add(kv):
            k, v = kv
            if v > need.get(k, 0):
                need[k] = v
        for r in reads:
            if r in self.last_w:
                add(self.last_w[r])
        for w in writes:
            if w in self.last_w:
                add(self.last_w[w])
            for kv in self.readers.get(w, ()):
                add(kv)
        waits = []
        for k, v in need.items():
            if k in self.cnt and k != eng and v > self.cnt[k]:
                self.future = getattr(self, "future", 0) + 1
                if self.future < 10:
                    print("WARNING future wait", eng, "on", k, v, "cnt", self.cnt[k], reads, writes)
            if self.known[eng].get(k, 0) >= v:
                continue
            self.known[eng][k] = v
            waits.append((k, v))
        return waits

    def _record(self, me, reads, writes):
        for r in reads:
            self.readers.setdefault(r, []).append(me)
        for w in writes:
            self.last_w[w] = me
            self.readers[w] = []

    def op(self, eng, fn, reads=(), writes=(), inc=True):
        waits = self._deps(eng, reads, writes)
        before = self.cnt[eng]
        if inc:
            self.cnt[eng] += 1
        me = (eng, before + 1)
        waits = [(k, v) for (k, v) in waits if not (k == eng and v > before)]
        self.ops[eng].append((waits, fn, inc, None))
        self._record(me, reads, writes)

    def dma(self, eng, out, in_, semkey, reads=(), writes=()):
        waits = self._deps(eng, reads, writes)
        self.dma_cnt[semkey] = self.dma_cnt.get(semkey, 0) + 16
        me = ("dma:" + semkey, self.dma_cnt[semkey])
        self.ops[eng].append((waits, lambda e, o=out, i=in_: e.dma_start(out=o, in_=i), False, semkey))
        self._record(me, reads, writes)

    def wait_all(self, eng, res_list):
        waits = self._deps(eng, res_list, [])
        self.ops[eng].append((waits, None, False, None))

    def barrier(self):
        for e in ENGS:
            waits = []
            for k in ENGS:
                if k != e and self.cnt[k] > self.known[e].get(k, 0):
                    self.known[e][k] = self.cnt[k]
                    waits.append((k, self.cnt[k]))
            for k, v in self.dma_cnt.items():
                kk = "dma:" + k
                if v > self.known[e].get(kk, 0):
                    self.known[e][kk] = v
                    waits.append((kk, v))
            self.ops[e].append((waits, None, False, None))

    def emit(self):
        nc = self.nc
        with contextlib.ExitStack() as st:
            sem = {}
            for e in ENGS:
                sem[e] = st.enter_context(nc.semaphore("s_" + e))
            for k in self.dma_cnt:
                sem["dma:" + k] = st.enter_context(nc.semaphore("d_" + k))
            block = st.enter_context(nc.Block())

            def run(engname, eh):
                for (waits, fn, inc, dsem) in self.ops[engname]:
                    for (k, v) in waits:
                        eh.wait_ge(sem[k], v)
                    if fn is None:
                        continue
                    ins = fn(eh)
                    if dsem is not None:
                        ins.then_inc(sem["dma:" + dsem], 16)
                    elif inc:
                        ins.then_inc(sem[engname], 1)

            @block.tensor
            def _(e):
                run("pe", e)

            @block.scalar
            def _(e):
                run("act", e)

            @block.vector
            def _(e):
                run("dve", e)

            @block.gpsimd
            def _(e):
                run("pool", e)

            @block.sync
            def _(e):
                run("sp", e)


class Builder:
    def __init__(self, n_sub=8, debug_raw=False):
        self.n_sub = n_sub
        self.debug_raw = debug_raw
        self.nc = bass.Bass("TRN2", target_bir_lowering=False)
        self.P = Prog(self.nc)
        self.psn = 0
        self.wi = 0
        self.wj = 0

    def din(self, name, shape):
        return self.nc.dram_tensor(name, list(shape), F32, kind="ExternalInput").ap()

    def dout(self, name, shape):
        return self.nc.dram_tensor(name, list(shape), F32, kind="ExternalOutput").ap()

    def sb(self, st, name, shape, dt):
        self.nalloc = getattr(self, "nalloc", 0) + 1
        return st.enter_context(self.nc.sbuf_tensor("%s_%d" % (name, self.nalloc), list(shape), dt))

    def bank(self, n=1):
        b = self.psn % 8
        if b + n > 8:
            b = 0
        self.psn = b + n
        return b, ["ps%d" % (b + i) for i in range(n)]

    def wload(self, src_list, kc, dst=None, dstkey=None):
        P = self.P
        s = self.wi % 2
        self.wi += 1
        skey = "ws%d" % s
        for (src, co) in src_list:
            ncols = src.shape[1]
            P.dma("sp", self.wstage[:, s, 0:kc, co:co + ncols], src.rearrange("(k p) n -> p k n", p=128), skey, writes=[skey])
        if dst is None:
            t = self.wj % 2
            self.wj += 1
            dkey = "wb%d" % t
            dstap = self.wbf[:, t, 0:kc, :]
            ret = self.wbf[:, t]
        else:
            dkey = dstkey
            dstap = dst
            ret = dst
        src_ap = self.wstage[:, s, 0:kc, :]
        P.op("pool", lambda e, o=dstap, i=src_ap: e.tensor_copy(out=o, in_=i), reads=[skey], writes=[dkey])
        return ret, dkey

    def mm(self, out, lhsT, rhs, start, stop, reads, writes, inc=None, **kw):
        if inc is None:
            inc = stop
        self.P.op("pe", lambda e: e.matmul(out, lhsT, rhs, start=start, stop=stop, **kw), reads=reads, writes=writes, inc=inc)

    def build(self):
        nc, P = self.nc, self.P
        din, dout = self.din, self.dout
        I = {}
        I["xT"] = din("xT", [D, T])
        I["c_fm"] = din("c_fm", [128, 8])
        I["S_in"] = din("S_in", [2, 128, 1024])
        I["halo_in"] = din("halo_in", [4, 128, 88])
        I["gm_w_in"] = din("gm_w_in", [2, D, 4096])
        I["gm_ln_g_bc"] = din("gm_ln_g_bc", [2, 128, 2048])
        I["gm_ln_b_bc"] = din("gm_ln_b_bc", [2, 128, 2048])
        I["gm_w_s"] = din("gm_w_s", [2, 8, 128, 128])
        I["gm_b_s"] = din("gm_b_s", [2, 1, 1024])
        I["gm_w_out"] = din("gm_w_out", [2, 2048, D])
        I["hg_w_in"] = din("hg_w_in", [2, D, 4096])
        I["hg_lb_fm"] = din("hg_lb_fm", [128, 16])
        I["hg_gn_g_fm"] = din("hg_gn_g_fm", [128, 16])
        I["hg_w_out"] = din("hg_w_out", [2, D, D])
        I["ffn_w_up"] = din("ffn_w_up", [4, D, 2 * FF])
        I["conv_w_fm"] = din("conv_w_fm", [128, 4 * 3 * 44])
        I["conv_b_fm"] = din("conv_b_fm", [128, 4 * 44])
        I["ffn_w_down"] = din("ffn_w_down", [4, FF, D])
        I["norm_g_fm"] = din("norm_g_fm", [128, 64])
        I["ada_w"] = din("ada_w", [4, D, 6 * D])
        I["ada_b_fm"] = din("ada_b_fm", [128, 4 * 48])
        I["final_g_fm"] = din("final_g_fm", [128, 8])
        I["ident"] = din("ident", [128, 128])
        I["scanmask"] = din("scanmask", [128, 512])
        I["cmask"] = din("cmask", [128, 4])
        self.trim = nc.dram_tensor("trimask", [128, 512], U32, kind="ExternalInput").ap()
        O = {}
        O["yT"] = dout("yT", [D, T])
        O["S_out"] = dout("S_out", [2, 128, 1024])
        O["halo_out"] = dout("halo_out", [4, 128, 88])
        self.I, self.O = I, O

        with contextlib.ExitStack() as st:
            sb = self.sb
            self.x = sb(st, "x", [128, DC, T], F32)
            self.wstage = sb(st, "wstage", [128, 2, 8, 256], F32)
            self.wbf = sb(st, "wbf", [128, 2, 8, 256], BF16)
            self.hb = sb(st, "hb", [128, DC, 1024], BF16)
            self.rstd = sb(st, "rstd", [128, 1024], F32)
            self.sq = sb(st, "sq", [128, 2, 512], BF16)
            self.ident = sb(st, "ident", [128, 128], F32)
            self.identb = sb(st, "identb", [128, 128], BF16)
            self.onesm = sb(st, "onesm", [128, 128], BF16)
            self.onesv = sb(st, "onesv", [128, 128], BF16)
            self.ones1 = sb(st, "ones1", [1, 128], BF16)
            self.cfm = sb(st, "cfm", [128, 8], F32)
            self.condb = sb(st, "condb", [128, 8], BF16)
            self.normg = sb(st, "normg", [128, 64], F32)
            self.finalg = sb(st, "finalg", [128, 8], F32)
            self.adab = sb(st, "adab", [128, 192], F32)
            self.convw = sb(st, "convw", [128, 4 * 3 * 44], F32)
            self.convb = sb(st, "convb", [128, 4 * 44], F32)
            self.lbraw = sb(st, "lbraw", [128, 16], F32)
            self.lb = sb(st, "lb", [128, 16], F32)
            self.oml = sb(st, "oml", [128, 16], F32)
            self.gng = sb(st, "gng", [128, 16], F32)
            self.mod = sb(st, "mod", [128, 48], F32)
            self.A = sb(st, "A", [128, 16], F32)
            self.halo = sb(st, "halo", [128, 2, 44, 2], F32)
            self.corr = sb(st, "corr", [128, 44, 2], F32)
            self.Sst = sb(st, "Sst", [128, 8, 128], F32)
            self.small = sb(st, "small", [128, 8], F32)
            self.epsc = sb(st, "epsc", [128, 1], F32)
            self.tmpn = sb(st, "tmpn", [128, 2, 512], F32)
            self.ps = st.enter_context(nc.psum_tensor("ps", [128, 8, 512], F32))
            x = self.x

            for c in range(DC):
                P.dma("sp", x[:, c, :], I["xT"][c * 128:(c + 1) * 128, :], "xin%d" % c, writes=["x%d_%d" % (c, j) for j in range(4)])
            cl = [(self.ident, I["ident"]), (self.cfm, I["c_fm"]),
                  (self.normg, I["norm_g_fm"]), (self.finalg, I["final_g_fm"]), (self.adab, I["ada_b_fm"]),
                  (self.convw, I["conv_w_fm"]), (self.convb, I["conv_b_fm"]), (self.lbraw, I["hg_lb_fm"]), (self.gng, I["hg_gn_g_fm"])]
            for (t_, d_) in cl:
                P.dma("sp", t_[:], d_, "cst", writes=["cst"])
            P.op("dve", lambda e: e.memset(self.epsc[:], EPS), writes=["epsc"])
            P.op("dve", lambda e: e.memset(self.onesm[:], 1.0 / 1024.0), writes=["onesm"])
            P.op("dve", lambda e: e.memset(self.onesv[:], 1.0 / 128.0), writes=["onesv"])
            P.op("dve", lambda e: e.memset(self.ones1[:], 1.0), writes=["ones1"])
            P.op("dve", lambda e: e.tensor_copy(out=self.identb[:], in_=self.ident[:]), reads=["cst"], writes=["identb"])
            P.op("act", lambda e: e.activation(out=self.small[:], in_=self.cfm[:], func=AF.Silu), reads=["cst"], writes=["small"])
            P.op("dve", lambda e: e.tensor_copy(out=self.condb[:], in_=self.small[:]), reads=["small"], writes=["condb"])
            P.op("dve", lambda e: e.memset(self.lb[:, 0:8], 0.0), writes=["lb"])
            P.op("dve", lambda e: e.tensor_tensor(out=self.lb[:, 8:16], in0=self.lbraw[:, 8:16], in1=self.lbraw[:, 0:8], op=ALU.subtract),
                 reads=["cst"], writes=["lb"])
            P.op("act", lambda e: e.activation(out=self.lb[:, 8:16], in_=self.lb[:, 8:16], func=AF.Sigmoid), reads=["lb"], writes=["lb"])
            P.op("dve", lambda e: e.tensor_scalar(out=self.oml[:], in0=self.lb[:], scalar1=-1.0, scalar2=1.0, op0=ALU.mult, op1=ALU.add),
                 reads=["lb"], writes=["oml"])

            sub = 0
            for layer in range(4):
                if sub >= self.n_sub:
                    break
                self.adaln(layer)
                if layer % 2 == 0:
                    self.gmlp(layer)
                else:
                    self.hgrn(layer)
                sub += 1
                if sub >= self.n_sub:
                    break
                self.ffn(layer)
                sub += 1
            self.final()
            P.emit()
        return nc

    def adaln(self, layer):
        P, I = self.P, self.I
        b0, keys = self.bank(1)
        pm = self.ps[:, b0, 0:48]
        for blk in range(24):
            w, wk = self.wload([(I["ada_w"][layer, :, blk * 256:(blk + 1) * 256], 0)], 8)
            for s in range(2):
                oc = blk * 2 + s
                for kc in range(8):
                    self.mm(pm[:, oc:oc + 1], w[:, kc, s * 128:(s + 1) * 128], self.condb[:, kc:kc + 1], kc == 0, kc == 7,
                            reads=[wk, "condb"], writes=keys, inc=(kc == 7))
        P.op("dve", lambda e: e.tensor_tensor(out=self.mod[:], in0=pm, in1=self.adab[:, layer * 48:(layer + 1) * 48], op=ALU.add),
             reads=keys + ["cst"], writes=["mod"])
        for n in range(2):
            sc = self.mod[:, 8 + 24 * n: 16 + 24 * n]
            ng = self.normg[:, layer * 16 + n * 8: layer * 16 + n * 8 + 8]
            P.op("dve", lambda e, sc=sc, ng=ng, n=n: e.scalar_tensor_tensor(out=self.A[:, n * 8:(n + 1) * 8], in0=sc, scalar=1.0, in1=ng,
                                                                            op0=ALU.add, op1=ALU.mult),
                 reads=["mod", "cst"], writes=["A"])

    def norm_h(self, n, t0, nt, gain=None, shift=None):
        P = self.P
        x = self.x
        for jt in range(nt):
            tl = t0 + jt
            ts = slice(tl * 512, (tl + 1) * 512)
            b0, keys = self.bank(1)
            for c in range(DC):
                P.op("act", lambda e, c=c, ts=ts: e.activation(out=self.sq[:, c % 2, :], in_=x[:, c, ts], func=AF.Square),
                     reads=["x%d_%d" % (c, tl)], writes=["sq%d" % (c % 2)])
                self.mm(self.ps[:, b0, :], self.onesm[:], self.sq[:, c % 2, :], c == 0, c == 7, reads=["onesm", "sq%d" % (c % 2)], writes=keys, inc=True)
            rs = self.rstd[:, jt * 512:(jt + 1) * 512]
            P.op("act", lambda e, rs=rs, b0=b0: e.activation(out=rs, in_=self.ps[:, b0, :], func=AF.Ln, bias=self.epsc[:, 0:1]), reads=keys + ["epsc"], writes=["rstd%d" % jt])
            P.op("act", lambda e, rs=rs: e.activation(out=rs, in_=rs, func=AF.Exp, scale=-0.5), reads=["rstd%d" % jt], writes=["rstd%d" % jt])
            if gain is None:
                continue
            for c in range(DC):
                tmp = self.tmpn[:, c % 2, :]
                P.op("dve", lambda e, c=c, ts=ts, tmp=tmp, rs=rs: e.scalar_tensor_tensor(out=tmp, in0=x[:, c, ts], scalar=gain[:, c:c + 1], in1=rs,
                                                                                         op0=ALU.mult, op1=ALU.mult),
                     reads=["x%d_%d" % (c, tl), "A", "rstd%d" % jt], writes=["tmpn%d" % (c % 2)])
                P.op("act", lambda e, c=c, tmp=tmp, jt=jt: e.activation(out=self.hb[:, c, jt * 512:(jt + 1) * 512], in_=tmp, func=AF.Identity,
                                                                        bias=shift[:, c:c + 1]),
                     reads=["tmpn%d" % (c % 2), "mod"], writes=["hb%d_%d" % (c, jt)])

    def resid(self, b0, nb, keys, c, tl0, gate):
        P, x = self.P, self.x
        ts = slice(tl0 * 512, (tl0 + nb) * 512)
        pv = self.ps[:, b0:b0 + nb, :].rearrange("p a b -> p (a b)")
        xk = ["x%d_%d" % (c, tl0 + i) for i in range(nb)]
        P.op("dve", lambda e: e.scalar_tensor_tensor(out=x[:, c, ts], in0=pv, scalar=gate[:, c:c + 1], in1=x[:, c, ts], op0=ALU.mult, op1=ALU.add),
             reads=keys + ["mod"] + xk, writes=xk)

    def gmlp(self, layer):
        P, I, nc = self.P, self.I, self.nc
        j = layer // 2
        P.barrier()
        with contextlib.ExitStack() as st:
            sb = self.sb
            wv = sb(st, "wv", [128, 8, 2048], BF16)
            ut = sb(st, "ut", [128, 16, 512], BF16)
            vf = sb(st, "vf", [128, 2048], F32)
            vn = sb(st, "vn", [128, 1, 2048], BF16)
            gbc = sb(st, "gbc", [128, 2048], F32)
            bbc = sb(st, "bbc", [128, 2048], F32)
            wsf = vf[:, 0:1024].rearrange("p (h m) -> p h m", h=8)
            wsT = sb(st, "wsT", [128, 8, 128], BF16)
            bsf = vf[0:1, 1024:2048]
            bsb = sb(st, "bsb", [1, 1024], BF16)
            stats = sb(st, "stats", [128, 4, 6], F32)
            mv = sb(st, "mv", [128, 2, 2], F32)
            rs2 = sb(st, "rs2", [128, 2, 2], F32)
            P.dma("sp", gbc[:], I["gm_ln_g_bc"][j], "gset", writes=["gset"])
            P.dma("sp", bbc[:], I["gm_ln_b_bc"][j], "gset", writes=["gset"])
            P.dma("sp", wsf, I["gm_w_s"][j].rearrange("h n m -> n h m"), "gset", writes=["gset"])
            P.dma("sp", bsf, I["gm_b_s"][j], "gset", writes=["gset"])
            P.op("dve", lambda e: e.tensor_copy(out=bsb[:], in_=bsf), reads=["gset"], writes=["bsb"])
            for h in range(8):
                b0, keys = self.bank(1)
                P.op("pe", lambda e, h=h, b0=b0: e.transpose(self.ps[:, b0, 0:128], wsf[:, h, :], self.ident[:]), reads=["gset", "cst"], writes=keys)
                P.op("dve", lambda e, h=h, b0=b0: e.tensor_copy(out=wsT[:, h, :], in_=self.ps[:, b0, 0:128]), reads=keys, writes=["wsT"])
            P.op("dve", lambda e: e.memset(wsT[64:128, :, 0:64], 0.0), reads=["wsT"], writes=["wsT"])
            for b in range(8):
                self.wload([(I["gm_w_in"][j, :, 2048 + b * 256: 2048 + (b + 1) * 256], 0)], 8, dst=wv[:, :, b * 256:(b + 1) * 256], dstkey="wv")
            P.barrier()
            sh1, g1 = self.mod[:, 0:8], self.mod[:, 16:24]
            for tl in range(4):
                self.norm_h(0, tl, 1, gain=self.A[:, 0:8], shift=sh1)
                hk = ["hb%d_0" % c for c in range(DC)]
                for b in range(8):
                    w, wk = self.wload([(I["gm_w_in"][j, :, b * 256:(b + 1) * 256], 0)], 8)
                    for s in range(2):
                        ch = 2 * b + s
                        b0, keys = self.bank(1)
                        for kc in range(8):
                            self.mm(self.ps[:, b0, :], w[:, kc, s * 128:(s + 1) * 128], self.hb[:, kc, 0:512], kc == 0, kc == 7, reads=[wk, "hb%d_0" % kc], writes=keys)
                        P.op("act", lambda e, ch=ch, b0=b0: e.activation(out=ut[:, ch, :], in_=self.ps[:, b0, :], func=AF.Gelu), reads=keys, writes=["ut%d" % ch])
                for bk in range(4):
                    tk = slice(bk * 128, (bk + 1) * 128)
                    par = bk % 2
                    for cg in range(4):
                        b0, keys = self.bank(1)
                        for kc in range(8):
                            self.mm(self.ps[:, b0, :], self.hb[:, kc, tk], wv[:, kc, cg * 512:(cg + 1) * 512], kc == 0, kc == 7, reads=["wv", "hb%d_0" % kc], writes=keys)
                        P.op("act", lambda e, cg=cg, b0=b0: e.activation(out=vf[:, cg * 512:(cg + 1) * 512], in_=self.ps[:, b0, :], func=AF.Gelu), reads=keys, writes=["vf%d" % cg])
                        P.op("dve", lambda e, cg=cg: e.bn_stats(out=stats[:, cg, :], in_=vf[:, cg * 512:(cg + 1) * 512]), reads=["vf%d" % cg], writes=["stats"])
                    P.op("dve", lambda e, par=par: e.bn_aggr(out=mv[:, par, :], in_=stats[:].rearrange("p a b -> p (a b)")), reads=["stats"], writes=["mv%d" % par])
                    P.op("act", lambda e, par=par: e.activation(out=rs2[:, par, 0:1], in_=mv[:, par, 1:2], func=AF.Ln, bias=self.epsc[:, 0:1]), reads=["mv%d" % par, "epsc"], writes=["rs2%d" % par])
                    P.op("act", lambda e, par=par: e.activation(out=rs2[:, par, 0:1], in_=rs2[:, par, 0:1], func=AF.Exp, scale=-0.5), reads=["rs2%d" % par], writes=["rs2%d" % par])
                    P.op("dve", lambda e, par=par: e.scalar_tensor_tensor(out=rs2[:, par, 1:2], in0=mv[:, par, 0:1], scalar=-1.0, in1=rs2[:, par, 0:1], op0=ALU.mult, op1=ALU.mult),
                         reads=["mv%d" % par, "rs2%d" % par], writes=["rs2%d" % par])
                    vfk = ["vf%d" % cg for cg in range(4)]
                    P.op("dve", lambda e, par=par: e.tensor_scalar(out=vf[:], in0=vf[:], scalar1=rs2[:, par, 0:1], scalar2=rs2[:, par, 1:2], op0=ALU.mult, op1=ALU.add),
                         reads=vfk + ["rs2%d" % par], writes=vfk)
                    P.op("pool", lambda e: e.tensor_tensor(out=vf[:], in0=vf[:], in1=gbc[:], op=ALU.mult), reads=vfk + ["gset"], writes=vfk)
                    P.op("pool", lambda e: e.tensor_tensor(out=vn[:, 0, :], in0=vf[:], in1=bbc[:], op=ALU.add), reads=vfk + ["gset"], writes=["vn"])
                    for q in range(4):
                        b0, keys = self.bank(1)
                        for dd in range(4):
                            dc = q * 4 + dd
                            hh = dc // 2
                            self.mm(self.ps[:, b0, dd * 128:(dd + 1) * 128], vn[:, 0, dc * 128:(dc + 1) * 128], wsT[:, hh, :], True, False,
                                    reads=["vn", "wsT"], writes=keys, inc=False)
                            self.mm(self.ps[:, b0, dd * 128:(dd + 1) * 128], self.ones1[0:1, :], bsb[0:1, hh * 128:(hh + 1) * 128], False, True,
                                    reads=["ones1", "bsb"], writes=keys, inc=(dd == 3))
                        uk = ["ut%d" % (q * 4 + dd) for dd in range(4)]
                        P.op("dve", lambda e, q=q, b0=b0, tk=tk: e.tensor_tensor(out=ut[:, q * 4:(q + 1) * 4, tk], in0=ut[:, q * 4:(q + 1) * 4, tk],
                                                                               in1=self.ps[:, b0, :].rearrange("p (a b) -> p a b", a=4), op=ALU.mult),
                             reads=keys + uk, writes=uk)
                for cb in range(4):
                    bk2 = [self.bank(1), self.bank(1)]
                    for kg in range(2):
                        w, wk = self.wload([(I["gm_w_out"][j, kg * 1024:(kg + 1) * 1024, cb * 256:(cb + 1) * 256], 0)], 8)
                        for s in range(2):
                            b0, keys = bk2[s]
                            for kc in range(8):
                                kk = kg * 8 + kc
                                self.mm(self.ps[:, b0, :], w[:, kc, s * 128:(s + 1) * 128], ut[:, kk, :], kk == 0, kk == 15, reads=[wk, "ut%d" % kk], writes=keys)
                    for s in range(2):
                        b0, keys = bk2[s]
                        self.resid(b0, 1, keys, 2 * cb + s, tl, g1)
        P.barrier()

    def ffn(self, layer):
        P, I = self.P, self.I
        P.barrier()
        with contextlib.ExitStack() as st:
            sb = self.sb
            hid = sb(st, "hid", [128, FC, 1024], BF16)
            Tg = sb(st, "Tg", [128, 2, 1024], F32)
            Tv = sb(st, "Tv", [128, 2, 1024], F32)
            sh2, g2 = self.mod[:, 24:32], self.mod[:, 40:48]
            cw = self.convw[:, layer * 132:(layer + 1) * 132].rearrange("p (j f) -> p j f", j=3)
            cb_ = self.convb[:, layer * 44:(layer + 1) * 44]
            P.dma("sp", self.halo[:, 0].rearrange("p f t -> p (f t)"), I["halo_in"][layer], "halo", writes=["halo0"])
            for hf in range(2):
                hin, hout = self.halo[:, hf % 2], self.halo[:, (hf + 1) % 2]
                hik, hok = "halo%d" % (hf % 2), "halo%d" % ((hf + 1) % 2)
                self.norm_h(1, 2 * hf, 2, gain=self.A[:, 8:16], shift=sh2)
                P.op("dve", lambda e, hin=hin: e.tensor_tensor(out=self.corr[:, :, 0], in0=cw[:, 0, :], in1=hin[:, :, 0], op=ALU.mult), reads=[hik, "cst"], writes=["corr"])
                P.op("dve", lambda e, hin=hin: e.tensor_tensor(out=self.corr[:, :, 1], in0=cw[:, 1, :], in1=hin[:, :, 1], op=ALU.mult), reads=[hik, "cst"], writes=["corr"])
                P.op("dve", lambda e: e.tensor_tensor(out=self.corr[:, :, 0], in0=self.corr[:, :, 0], in1=self.corr[:, :, 1], op=ALU.add), reads=["corr"], writes=["corr"])
                P.op("dve", lambda e, hin=hin: e.tensor_tensor(out=self.corr[:, :, 1], in0=cw[:, 0, :], in1=hin[:, :, 1], op=ALU.mult), reads=[hik, "cst", "corr"], writes=["corr"])
                for gb in range(11):
                    wG, wGk = self.wload([(I["ffn_w_up"][layer, :, gb * 256:(gb + 1) * 256], 0)], 8)
                    wV, wVk = self.wload([(I["ffn_w_up"][layer, :, FF + gb * 256: FF + (gb + 1) * 256], 0)], 8)
                    for s in range(2):
                        g = 2 * gb + s
                        par = g % 2
                        for (w, wk, f, Tb, tk) in ((wG, wGk, g, Tg, "Tg%d" % par), (wV, wVk, FC + g, Tv, "Tv%d" % par)):
                            b0, keys = self.bank(2)
                            for kc in range(8):
                                for tl in range(2):
                                    self.mm(self.ps[:, b0 + tl, :], w[:, kc, s * 128:(s + 1) * 128], self.hb[:, kc, tl * 512:(tl + 1) * 512], kc == 0, kc == 7,
                                            reads=[wk, "hb%d_%d" % (kc, tl)], writes=[keys[tl]], inc=(kc == 7 and tl == 1))
                            pv = self.ps[:, b0:b0 + 2, :].rearrange("p a b -> p (a b)")
                            Tt = Tb[:, par, :]
                            P.op("act", lambda e, pv=pv, Tt=Tt, f=f: e.activation(out=Tt, in_=pv, func=AF.Identity, bias=cb_[:, f:f + 1], scale=cw[:, 2, f:f + 1]),
                                 reads=keys + ["cst"], writes=[tk])
                            P.op("act", lambda e, pv=pv, f=f, hout=hout: e.activation(out=hout[:, f, :], in_=pv[:, 1022:1024], func=AF.Identity), reads=keys, writes=[hok])
                            P.op("dve", lambda e, pv=pv, Tt=Tt, f=f: e.scalar_tensor_tensor(out=Tt[:, 1:1024], in0=pv[:, 0:1023], scalar=cw[:, 1, f:f + 1], in1=Tt[:, 1:1024],
                                                                                           op0=ALU.mult, op1=ALU.add), reads=keys + ["cst", tk], writes=[tk])
                            P.op("dve", lambda e, pv=pv, Tt=Tt, f=f: e.scalar_tensor_tensor(out=Tt[:, 2:1024], in0=pv[:, 0:1022], scalar=cw[:, 0, f:f + 1], in1=Tt[:, 2:1024],
                                                                                           op0=ALU.mult, op1=ALU.add), reads=keys + ["cst", tk], writes=[tk])
                            P.op("dve", lambda e, Tt=Tt, f=f: e.tensor_tensor(out=Tt[:, 0:2], in0=Tt[:, 0:2], in1=self.corr[:, f, :], op=ALU.add), reads=["corr", tk], writes=[tk])
                        P.op("act", lambda e, par=par: e.activation(out=Tg[:, par, :], in_=Tg[:, par, :], func=AF.Gelu), reads=["Tg%d" % par], writes=["Tg%d" % par])
                        P.op("pool", lambda e, par=par, g=g: e.tensor_tensor(out=hid[:, g, :], in0=Tg[:, par, :], in1=Tv[:, par, :], op=ALU.mult),
                             reads=["Tg%d" % par, "Tv%d" % par], writes=["hid%d" % g])
                for cb in range(4):
                    bk2 = [self.bank(2), self.bank(2)]
                    for kg in range(3):
                        kcn = 8 if kg < 2 else 6
                        w, wk = self.wload([(I["ffn_w_down"][layer, kg * 1024: kg * 1024 + kcn * 128, cb * 256:(cb + 1) * 256], 0)], kcn)
                        for s in range(2):
                            b0, keys = bk2[s]
                            for kc in range(kcn):
                                kk = kg * 8 + kc
                                for tl in range(2):
                                    self.mm(self.ps[:, b0 + tl, :], w[:, kc, s * 128:(s + 1) * 128], hid[:, kk, tl * 512:(tl + 1) * 512], kk == 0, kk == FC - 1,
                                            reads=[wk, "hid%d" % kk], writes=[keys[tl]], inc=(kc == kcn - 1 and tl == 1))
                    for s in range(2):
                        b0, keys = bk2[s]
                        self.resid(b0, 2, keys, 2 * cb + s, 2 * hf, g2)
            P.dma("sp", self.O["halo_out"][layer], self.halo[:, 0].rearrange("p f t -> p (f t)"), "hout", reads=["halo0"], writes=["halo_out"])
        P.barrier()

    def hgrn(self, layer):
        P, I = self.P, self.I
        j = layer // 2
        P.barrier()
        with contextlib.ExitStack() as st:
            sb = self.sb
            vtok = sb(st, "vtok", [128, 8, 1024], BF16)
            ypre = sb(st, "ypre", [128, 8, 1024], BF16)
            Tq = sb(st, "Tq", [128, 512], F32)
            Tf = sb(st, "Tf", [128, 512], F32)
            Tk = sb(st, "Tk", [128, 512], F32)
            Tb = sb(st, "Tb", [128, 512], F32)
            Td = sb(st, "Td", [128, 512], F32)
            Tl = sb(st, "Tl", [128, 512], F32)
            TB = sb(st, "TB", [128, 512], F32)
            E1 = sb(st, "E1", [128, 512], F32)
            E2 = Td
            E3 = sb(st, "E3", [128, 512], F32)
            E4 = TB
            Tg_ = sb(st, "Tgs", [128, 512], F32)
            To = Tl
            q1 = sb(st, "q1", [128, 512], BF16)
            k1 = sb(st, "k1", [128, 512], BF16)
            q2 = sb(st, "q2", [128, 512], BF16)
            k2 = sb(st, "k2", [128, 512], BF16)
            k2T = sb(st, "k2T", [128, 4, 4, 128], BF16)
            Am = sb(st, "Am", [128, 512], BF16)
            osq = sb(st, "osq", [128, 512], BF16)
            ro = sb(st, "ro", [128, 512], F32)
            Sh = sb(st, "Sh", [128, 17, 128], F32)
            Sb = sb(st, "Sb", [128, 16, 128], BF16)
            P.op("dve", lambda e: e.memset(Am[:], 0.0), writes=["Am"])
            scanmask = sb(st, "scanmask", [128, 512], F32)
            trimask = sb(st, "trimask", [128, 512], U32)
            P.dma("sp", scanmask[:], I["scanmask"], "hset", writes=["hset"])
            cmask = sb(st, "cmask", [128, 4], F32)
            P.dma("sp", cmask[:], I["cmask"], "hset", writes=["hset"])
            P.dma("sp", trimask[:], self.trim, "hset", writes=["hset"])
            P.dma("sp", self.Sst[:].rearrange("p h v -> p (h v)"), I["S_in"][j], "sin", writes=["Sst%d" % h for h in range(8)])
            sh1, g1 = self.mod[:, 0:8], self.mod[:, 16:24]
            W = I["hg_w_in"]
            for hf in range(2):
                self.norm_h(0, 2 * hf, 2, gain=self.A[:, 0:8], shift=sh1)
                for b in range(4):
                    w, wk = self.wload([(W[j, :, 2048 + b * 256: 2048 + (b + 1) * 256], 0)], 8)
                    for bp in range(4):
                        b0, keys = self.bank(1)
                        for s in range(2):
                            bk = 2 * bp + s
                            for kc in range(8):
                                self.mm(self.ps[:, b0, s * 256:(s + 1) * 256], self.hb[:, kc, bk * 128:(bk + 1) * 128], w[:, kc, :], kc == 0, kc == 7,
                                        reads=[wk, "hb%d_%d" % (kc, bk // 4)], writes=keys, inc=(kc == 7 and s == 1))
                        P.op("act", lambda e, b=b, bp=bp, b0=b0: e.activation(out=vtok[:, 2 * bp:2 * bp + 2, b * 256:(b + 1) * 256],
                                                                            in_=self.ps[:, b0, :].rearrange("p (a b) -> p a b", a=2), func=AF.Identity),
                             reads=keys, writes=["vtok"])
                for hd in range(8):
                    wA, wAk = self.wload([(W[j, :, hd * 128:(hd + 1) * 128], 0), (W[j, :, 1024 + hd * 128: 1024 + (hd + 1) * 128], 128)], 8)
                    if hd % 2 == 0:
                        wB, wBk = self.wload([(W[j, :, 3072 + hd * 128: 3072 + (hd + 2) * 128], 0)], 8)
                    lbh, omlh = self.lb[:, j * 8 + hd: j * 8 + hd + 1], self.oml[:, j * 8 + hd: j * 8 + hd + 1]
                    gnh = self.gng[:, j * 8 + hd: j * 8 + hd + 1]
                    for tl in range(2):
                        ts = slice(tl * 512, (tl + 1) * 512)
                        hk = ["hb%d_%d" % (kc, tl) for kc in range(8)]
                        bq, kq = self.bank(1)
                        bf, kf = self.bank(1)
                        bg, kg_ = self.bank(1)
                        for (b0, keys, w, wk, c0) in ((bq, kq, wA, wAk, 0), (bf, kf, wA, wAk, 128), (bg, kg_, wB, wBk, (hd % 2) * 128)):
                            for kc in range(8):
                                self.mm(self.ps[:, b0, :], w[:, kc, c0:c0 + 128], self.hb[:, kc, ts], kc == 0, kc == 7, reads=[wk, hk[kc]], writes=keys)
                        pq, pf, pg = self.ps[:, bq, :], self.ps[:, bf, :], self.ps[:, bg, :]
                        P.op("act", lambda e, pq=pq: e.activation(out=Tq[:], in_=pq, func=AF.Sigmoid), reads=kq, writes=["Tq"])
                        P.op("act", lambda e, pf=pf: e.activation(out=Tf[:], in_=pf, func=AF.Sigmoid), reads=kf, writes=["Tf"])
                        P.op("act", lambda e, pg=pg: e.activation(out=Tg_[:], in_=pg, func=AF.Sigmoid), reads=kg_, writes=["Tgs"])
                        P.op("dve", lambda e, pq=pq: e.tensor_tensor(out=Tq[:], in0=Tq[:], in1=pq, op=ALU.mult), reads=kq + ["Tq"], writes=["Tq"])
                        P.op("dve", lambda e, pg=pg: e.tensor_tensor(out=Tg_[:], in0=Tg_[:], in1=pg, op=ALU.mult), reads=kg_ + ["Tgs"], writes=["Tgs"])
                        P.op("dve", lambda e, lbh=lbh, omlh=omlh: e.tensor_scalar(out=Tf[:], in0=Tf[:], scalar1=omlh, scalar2=lbh, op0=ALU.mult, op1=ALU.add),
                             reads=["Tf", "lb", "oml"], writes=["Tf"])
                        P.op("pool", lambda e: e.tensor_scalar(out=Tk[:], in0=Tf[:], scalar1=-1.0, scalar2=1.0, op0=ALU.mult, op1=ALU.add), reads=["Tf"], writes=["Tk"])
                        P.op("act", lambda e: e.activation(out=Tl[:], in_=Tf[:], func=AF.Ln), reads=["Tf"], writes=["Tl"])
                        P.op("dve", lambda e: e.tensor_tensor_scan(out=Tb[:], data0=scanmask[:], data1=Tl[:], initial=0.0, op0=ALU.mult, op1=ALU.add),
                             reads=["Tl", "hset"], writes=["Tb"])
                        b3 = Tb[:].rearrange("p (c t) -> p c t", t=32)
                        P.op("dve", lambda e, b3=b3: e.tensor_tensor(out=Td[:].rearrange("p (c t) -> p c t", t=32), in0=b3, in1=b3[:, :, 15:16].to_broadcast([128, 16, 32]), op=ALU.subtract),
                             reads=["Tb"], writes=["Td"])
                        P.op("dve", lambda e, b3=b3: e.tensor_tensor(out=TB[:].rearrange("p (c t) -> p c t", t=32), in0=b3[:, :, 31:32].to_broadcast([128, 16, 32]), in1=b3, op=ALU.subtract),
                             reads=["Tb"], writes=["TB"])
                        P.op("act", lambda e: e.activation(out=E1[:], in_=Td[:], func=AF.Exp), reads=["Td"], writes=["E1"])
                        P.op("act", lambda e: e.activation(out=E2[:], in_=Td[:], func=AF.Exp, scale=-1.0), reads=["Td", "E1"], writes=["Td"])
                        P.op("act", lambda e: e.activation(out=E3[:], in_=Tb[:], func=AF.Exp), reads=["Tb"], writes=["E3"])
                        P.op("act", lambda e: e.activation(out=E4[:], in_=TB[:], func=AF.Exp), reads=["TB"], writes=["TB"])
                        P.op("pool", lambda e: e.tensor_tensor(out=q1[:], in0=Tq[:], in1=E1[:], op=ALU.mult), reads=["Tq", "E1"], writes=["q1"])
                        P.op("pool", lambda e: e.tensor_tensor(out=k1[:], in0=Tk[:], in1=E2[:], op=ALU.mult), reads=["Tk", "Td"], writes=["k1"])
                        P.op("pool", lambda e: e.tensor_tensor(out=q2[:], in0=Tq[:], in1=E3[:], op=ALU.mult), reads=["Tq", "E3"], writes=["q2"])
                        P.op("pool", lambda e: e.tensor_tensor(out=k2[:], in0=Tk[:], in1=E4[:], op=ALU.mult), reads=["Tk", "TB"], writes=["k2"])
                        bt, kt = self.bank(1)
                        ptb = self.ps[:, bt, :].bitcast(BF16)
                        for bl in range(4):
                            P.op("pe", lambda e, bl=bl, ptb=ptb: e.transpose(ptb[:, bl * 128:(bl + 1) * 128], k2[:, bl * 128:(bl + 1) * 128], self.identb[:]),
                                 reads=["k2", "identb"], writes=kt, inc=(bl == 3))
                        for q4 in range(4):
                            P.op("act", lambda e, ptb=ptb, q4=q4: e.activation(out=k2T[:, q4], in_=ptb[:, 0:512].rearrange("p (a b) -> p a b", a=4), func=AF.Identity, scale=cmask[:, q4:q4 + 1]),
                                 reads=kt + ["hset"], writes=["k2T"])
                        P.op("dve", lambda e, hd=hd: e.tensor_copy(out=Sh[:, 0, :], in_=self.Sst[:, hd, :]), reads=["Sst%d" % hd], writes=["Sh"])
                        for grp in range(4):
                            bd, kd = self.bank(1)
                            for q4 in range(4):
                                ch = grp * 4 + q4
                                bl = grp
                                gbl = tl * 4 + bl
                                self.mm(self.ps[:, bd, q4 * 128:(q4 + 1) * 128], k2T[:, q4, bl, :], vtok[:, gbl, hd * 128:(hd + 1) * 128], True, True,
                                        reads=["k2T", "vtok"], writes=kd, inc=(q4 == 3))
                            for q4 in range(4):
                                ch = grp * 4 + q4
                                P.op("dve", lambda e, ch=ch, bd=bd, q4=q4: e.scalar_tensor_tensor(out=Sh[:, ch + 1, :], in0=Sh[:, ch, :], scalar=E3[:, ch * 32 + 31: ch * 32 + 32],
                                                                                                 in1=self.ps[:, bd, q4 * 128:(q4 + 1) * 128], op0=ALU.mult, op1=ALU.add),
                                     reads=kd + ["Sh", "E3"], writes=["Sh"])
                        P.op("act", lambda e: e.activation(out=Sb[:], in_=Sh[:, 0:16, :], func=AF.Identity), reads=["Sh"], writes=["Sb"])
                        P.op("dve", lambda e, hd=hd: e.tensor_copy(out=self.Sst[:, hd, :], in_=Sh[:, 16, :]), reads=["Sh"], writes=["Sst%d" % hd])
                        ba, ka = self.bank(1)
                        for bl in range(4):
                            sl = slice(bl * 128, (bl + 1) * 128)
                            self.mm(self.ps[:, ba, sl], k1[:, sl], q1[:, sl], True, True, reads=["k1", "q1"], writes=ka, inc=(bl == 3))
                        P.op("dve", lambda e, ba=ba: e.copy_predicated(out=Am[:], mask=trimask[:], data=self.ps[:, ba, :]), reads=ka + ["hset", "Am"], writes=["Am"])
                        bo, ko = self.bank(1)
                        for bl in range(4):
                            sl = slice(bl * 128, (bl + 1) * 128)
                            gbl = tl * 4 + bl
                            self.mm(self.ps[:, bo, sl], vtok[:, gbl, hd * 128:(hd + 1) * 128], Am[:, sl], True, False, reads=["vtok", "Am"], writes=ko, inc=False)
                            for q4 in range(4):
                                ch = bl * 4 + q4
                                cs = slice(ch * 32, (ch + 1) * 32)
                                self.mm(self.ps[:, bo, cs], Sb[:, ch, :], q2[:, cs], False, q4 == 3, reads=["Sb", "q2"], writes=ko, inc=(q4 == 3 and bl == 3))
                        po = self.ps[:, bo, :]
                        P.op("act", lambda e, po=po: e.activation(out=osq[:], in_=po, func=AF.Square), reads=ko, writes=["osq"])
                        bm, km = self.bank(1)
                        self.mm(self.ps[:, bm, :], self.onesv[:], osq[:], True, True, reads=["onesv", "osq"], writes=km)
                        P.op("act", lambda e, bm=bm: e.activation(out=ro[:], in_=self.ps[:, bm, :], func=AF.Ln, bias=self.epsc[:, 0:1]), reads=km + ["epsc"], writes=["ro"])
                        P.op("act", lambda e: e.activation(out=ro[:], in_=ro[:], func=AF.Exp, scale=-0.5), reads=["ro"], writes=["ro"])
                        P.op("dve", lambda e, po=po: e.tensor_tensor(out=To[:], in0=ro[:], in1=po, op=ALU.mult), reads=ko + ["ro"], writes=["Tl"])
                        P.op("dve", lambda e, gnh=gnh, hd=hd, ts=ts: e.scalar_tensor_tensor(out=ypre[:, hd, ts], in0=To[:], scalar=gnh, in1=Tg_[:], op0=ALU.mult, op1=ALU.mult),
                             reads=["Tl", "Tgs", "cst"], writes=["ypre%d" % hd])
                for cb in range(4):
                    w, wk = self.wload([(I["hg_w_out"][j, :, cb * 256:(cb + 1) * 256], 0)], 8)
                    for s in range(2):
                        b0, keys = self.bank(2)
                        for kc in range(8):
                            for tl in range(2):
                                self.mm(self.ps[:, b0 + tl, :], w[:, kc, s * 128:(s + 1) * 128], ypre[:, kc, tl * 512:(tl + 1) * 512], kc == 0, kc == 7,
                                        reads=[wk, "ypre%d" % kc], writes=[keys[tl]], inc=(kc == 7 and tl == 1))
                        self.resid(b0, 2, keys, 2 * cb + s, 2 * hf, g1)
            P.dma("sp", self.O["S_out"][j], self.Sst[:].rearrange("p h v -> p (h v)"), "sout", reads=["Sst%d" % h for h in range(8)], writes=["S_out"])
        P.barrier()

    def final(self):
        P = self.P
        P.barrier()
        with contextlib.ExitStack() as st:
            sb = self.sb
            yo = sb(st, "yo", [128, DC, T], F32)
            for hf in range(2):
                if not self.debug_raw:
                    self.norm_h(0, 2 * hf, 2)
                for c in range(DC):
                    for tl in range(2):
                        tg = 2 * hf + tl
                        src = self.x[:, c, tg * 512:(tg + 1) * 512]
                        dst = yo[:, c, tg * 512:(tg + 1) * 512]
                        if self.debug_raw:
                            P.op("dve", lambda e, src=src, dst=dst: e.tensor_copy(out=dst, in_=src), reads=["x%d_%d" % (c, tg)], writes=["yo%d" % c])
                        else:
                            P.op("dve", lambda e, src=src, dst=dst, c=c, tl=tl: e.scalar_tensor_tensor(out=dst, in0=src, scalar=self.finalg[:, c:c + 1],
                                                                                                    in1=self.rstd[:, tl * 512:(tl + 1) * 512], op0=ALU.mult, op1=ALU.mult),
                                 reads=["x%d_%d" % (c, tg), "rstd%d" % tl, "cst"], writes=["yo%d" % c])
                    P.dma("sp", self.O["yT"][c * 128:(c + 1) * 128, hf * 1024:(hf + 1) * 1024], yo[:, c, hf * 1024:(hf + 1) * 1024], "yout", reads=["yo%d" % c], writes=["yT"])
            P.wait_all("sp", ["yT", "S_out", "halo_out"])
        P.barrier()


def _fm(v, ncol):
    return np.ascontiguousarray(np.asarray(v, np.float32).reshape(ncol, 128).T)


def _shared_inputs(inp):
    f = lambda a: np.ascontiguousarray(np.asarray(a, np.float32))
    sh = {}
    for k in ["gm_w_in", "gm_w_s", "gm_w_out", "hg_w_in", "hg_w_out", "ffn_w_up", "ffn_w_down", "ada_w"]:
        sh[k] = f(inp[k])
    sh["gm_ln_g_bc"] = np.ascontiguousarray(np.broadcast_to(f(inp["gm_ln_g"])[:, None, :], (2, 128, 2048)))
    sh["gm_ln_b_bc"] = np.ascontiguousarray(np.broadcast_to(f(inp["gm_ln_b"])[:, None, :], (2, 128, 2048)))
    sh["gm_b_s"] = f(inp["gm_b_s"]).reshape(2, 1, 1024)
    sh["hg_lb_fm"] = np.concatenate([_fm(inp["hg_lb"][j], 8) for j in range(2)], 1)
    sh["hg_gn_g_fm"] = np.concatenate([_fm(inp["hg_gn_g"][j], 8) for j in range(2)], 1)
    sh["conv_w_fm"] = np.concatenate([_fm(inp["ffn_conv_w"][l, jj], 44) for l in range(4) for jj in range(3)], 1)
    sh["conv_b_fm"] = np.concatenate([_fm(inp["ffn_conv_b"][l], 44) for l in range(4)], 1)
    sh["norm_g_fm"] = np.concatenate([_fm(inp["norm_g"][l, n], 8) for l in range(4) for n in range(2)], 1)
    sh["ada_b_fm"] = np.concatenate([_fm(inp["ada_b"][l], 48) for l in range(4)], 1)
    sh["final_g_fm"] = _fm(inp["final_g"], 8)
    sh["ident"] = np.eye(128, dtype=np.float32)
    sm = np.ones((128, 512), np.float32)
    sm[:, ::32] = 0.0
    sh["scanmask"] = sm
    sh["cmask"] = (np.arange(128)[:, None] // 32 == np.arange(4)[None, :]).astype(np.float32)
    jj, ii = np.meshgrid(np.arange(128), np.arange(128), indexing="ij")
    tri = ((jj // 32 == ii // 32) & (jj <= ii)).astype(np.uint32)
    sh["trimask"] = np.ascontiguousarray(np.tile(tri, (1, 4)))
    return sh


_NC_CACHE = {}


def _get_nc(n_sub=8, debug_raw=False):
    key = (n_sub, debug_raw)
    if key not in _NC_CACHE:
        _NC_CACHE[key] = Builder(n_sub, debug_raw).build()
    return _NC_CACHE[key]


def kernel(n_sub=8, debug_raw=False, **inp):
    x = np.asarray(inp["x"], np.float32)
    c = np.asarray(inp["c"], np.float32)
    B = x.shape[0]
    sh = _shared_inputs(inp)
    nc = _get_nc(n_sub, debug_raw)
    out = np.zeros((B, 4096, D), np.float32)
    S_prev = [np.zeros((2, 128, 1024), np.float32) for _ in range(B)]
    H_prev = [np.zeros((4, 128, 88), np.float32) for _ in range(B)]
    for half in range(2):
        in_maps = []
        for b in range(B):
            m = dict(sh)
            m["xT"] = np.ascontiguousarray(x[b, half * T:(half + 1) * T, :].T)
            m["c_fm"] = _fm(c[b], 8)
            m["S_in"] = S_prev[b]
            m["halo_in"] = H_prev[b]
            in_maps.append(m)
        res = run_bass_kernel_spmd(nc, in_maps, core_ids=list(range(B)))
        for b in range(B):
            r = res.results[b]
            out[b, half * T:(half + 1) * T, :] = np.asarray(r["yT"]).T
            S_prev[b] = np.ascontiguousarray(np.asarray(r["S_out"], np.float32))
            H_prev[b] = np.ascontiguousarray(np.asarray(r["halo_out"], np.float32))
    return out
```
